# Optimizing a Trainium2 kernel written in Bass

```python
import jax
import jax.numpy as jnp
from jax import lax
import numpy as np

D_MODEL = 2048
BATCH = 4
SEQ = 2048
DEPTH = 4
DEC_BATCH = 2
DEC_SEQ = 8192
PAST_LEN = 128

D_MIX = D_MODEL
CONV_CH = D_MIX // 2
CONV_WIDTH = 31
CONV_PAD = CONV_WIDTH // 2
V_HEAD_DIM = 128
MLA_HEADS = (D_MIX - CONV_CH) // V_HEAD_DIM
QK_NOPE_DIM = 128
QK_ROPE_DIM = 64
Q_LORA_RANK = 512
KV_LORA_RANK = 256
ROPE_THETA = 10000.0
D_IN = 2 * CONV_CH + Q_LORA_RANK + KV_LORA_RANK + QK_ROPE_DIM
D_FF = 5632
N_MEM = 256
X_HEADS = 4
X_HEAD_DIM = D_MODEL // X_HEADS
Q_BLOCK = 128
EPS = 1e-6

kernel_name = 'hybrid_conformer_mla_macaron_encoder'


def rms_norm(x, g):
    x32 = x.astype(jnp.float32)
    y = x32 * lax.rsqrt(jnp.mean(x32 * x32, axis=-1, keepdims=True) + EPS)
    return y.astype(x.dtype) * g


def layer_norm(x, g, b):
    x32 = x.astype(jnp.float32)
    xc = x32 - jnp.mean(x32, axis=-1, keepdims=True)
    y = xc * lax.rsqrt(jnp.mean(xc * xc, axis=-1, keepdims=True) + EPS)
    return y.astype(x.dtype) * g + b


def swiglu(h, w_gate, w_up, w_down):
    return (jax.nn.silu(h @ w_gate) * (h @ w_up)) @ w_down


def rope_tables(seq, dtype):
    inv = 1.0 / (ROPE_THETA ** (jnp.arange(0, QK_ROPE_DIM, 2, dtype=jnp.float32) / QK_ROPE_DIM))
    ang = jnp.arange(seq, dtype=jnp.float32)[:, None] * inv[None, :]
    ang = jnp.concatenate([ang, ang], axis=-1)
    return jnp.cos(ang).astype(dtype), jnp.sin(ang).astype(dtype)


def apply_rope(x, cos, sin):
    x1, x2 = jnp.split(x, 2, axis=-1)
    return x * cos + jnp.concatenate([-x2, x1], axis=-1) * sin


def conv_module(u, dw_w, dw_b, ln_g, ln_b):
    a, b = jnp.split(u, 2, axis=-1)
    h = a * jax.nn.sigmoid(b)
    h = lax.conv_general_dilated(
        h, dw_w[:, None, :], window_strides=(1,), padding=[(CONV_PAD, CONV_PAD)],
        dimension_numbers=('NWC', 'WIO', 'NWC'), feature_group_count=CONV_CH) + dw_b
    return jax.nn.silu(layer_norm(h, ln_g, ln_b))


def mla_block_attention(q_nope, q_rope, k_nope, k_rope, v):
    bsz, seq = q_nope.shape[0], q_nope.shape[1]
    nblk = seq // Q_BLOCK
    scale = (QK_NOPE_DIM + QK_ROPE_DIM) ** -0.5

    def to_blocks(t):
        return jnp.moveaxis(t.reshape((bsz, nblk, Q_BLOCK) + t.shape[2:]), 1, 0)

    def one_block(qs):
        qn, qr = qs
        s = (jnp.einsum('bqhd,bkhd->bhqk', qn, k_nope, preferred_element_type=jnp.float32)
             + jnp.einsum('bqhr,bkr->bhqk', qr, k_rope, preferred_element_type=jnp.float32))
        p = jax.nn.softmax(s * scale, axis=-1).astype(v.dtype)
        return jnp.einsum('bhqk,bkhd->bqhd', p, v)

    out = lax.map(one_block, (to_blocks(q_nope), to_blocks(q_rope)))
    return jnp.moveaxis(out, 0, 1).reshape(bsz, seq, MLA_HEADS * V_HEAD_DIM)


def parallel_mixer(h, w_in, dw_w, dw_b, ln_g, ln_b, g_q_lat, g_kv_lat, w_q_up, w_kv_up,
                   g_group, w_out, cos, sin):
    bsz, seq, _ = h.shape
    u = h @ w_in
    o1 = 2 * CONV_CH
    o2 = o1 + Q_LORA_RANK
    o3 = o2 + KV_LORA_RANK
    conv_out = conv_module(u[..., :o1], dw_w, dw_b, ln_g, ln_b)
    q = (rms_norm(u[..., o1:o2], g_q_lat) @ w_q_up).reshape(
        bsz, seq, MLA_HEADS, QK_NOPE_DIM + QK_ROPE_DIM)
    kv = (rms_norm(u[..., o2:o3], g_kv_lat) @ w_kv_up).reshape(
        bsz, seq, MLA_HEADS, QK_NOPE_DIM + V_HEAD_DIM)
    q_nope = q[..., :QK_NOPE_DIM]
    q_rope = apply_rope(q[..., QK_NOPE_DIM:], cos[:, None, :], sin[:, None, :])
    k_nope = kv[..., :QK_NOPE_DIM]
    v = kv[..., QK_NOPE_DIM:]
    k_rope = apply_rope(u[..., o3:], cos, sin)
    attn_out = mla_block_attention(q_nope, q_rope, k_nope, k_rope, v)
    y = jnp.concatenate([rms_norm(conv_out, g_group[:CONV_CH]),
                         rms_norm(attn_out, g_group[CONV_CH:])], axis=-1)
    return y @ w_out


def memory_cross_attention(h, m, w_q, w_k, w_v, w_o):
    bsz, seq, _ = h.shape
    n_mem = m.shape[1]
    q = (h @ w_q).reshape(bsz, seq, X_HEADS, X_HEAD_DIM)
    k = (m @ w_k).reshape(bsz, n_mem, X_HEADS, X_HEAD_DIM)
    v = (m @ w_v).reshape(bsz, n_mem, X_HEADS, X_HEAD_DIM)
    s = jnp.einsum('bqhd,bkhd->bhqk', q, k, preferred_element_type=jnp.float32) * (X_HEAD_DIM ** -0.5)
    p = jax.nn.softmax(s, axis=-1).astype(v.dtype)
    o = jnp.einsum('bhqk,bkhd->bqhd', p, v).reshape(bsz, seq, X_HEADS * X_HEAD_DIM)
    return o @ w_o


def encoder_trunk(x, mem, p):
    cos, sin = rope_tables(x.shape[1], x.dtype)
    for l in range(DEPTH):
        h = rms_norm(x, p['g_ffn1_pre'][l])
        f = swiglu(h, p['w_ffn1_gate'][l], p['w_ffn1_up'][l], p['w_ffn1_down'][l])
        x = x + 0.5 * rms_norm(f, p['g_ffn1_post'][l])
        h = rms_norm(x, p['g_mix_pre'][l])
        t = parallel_mixer(h, p['w_in'][l], p['conv_dw_w'][l], p['conv_dw_b'][l],
                           p['conv_ln_g'][l], p['conv_ln_b'][l], p['g_q_lat'][l], p['g_kv_lat'][l],
                           p['w_q_up'][l], p['w_kv_up'][l], p['g_group_out'][l], p['w_out'][l],
                           cos, sin)
        x = x + rms_norm(t, p['g_mix_post'][l])
        h = rms_norm(x, p['g_x_pre'][l])
        m = rms_norm(mem, p['g_mem'][l])
        c = memory_cross_attention(h, m, p['w_xq'][l], p['w_xk'][l], p['w_xv'][l], p['w_xo'][l])
        x = x + rms_norm(c, p['g_x_post'][l])
        h = rms_norm(x, p['g_ffn2_pre'][l])
        f = swiglu(h, p['w_ffn2_gate'][l], p['w_ffn2_up'][l], p['w_ffn2_down'][l])
        x = x + 0.5 * rms_norm(f, p['g_ffn2_post'][l])
    return x


def setup_inputs(seed: int = 0) -> dict:
    key = jax.random.key(seed)
    keys = jax.random.split(key, 40)
    ctr = [0]

    def nk():
        ctr[0] += 1
        return keys[ctr[0] - 1]

    def nrm(shape, scale):
        return jax.random.normal(nk(), shape, jnp.float32) * scale

    def gain(n):
        return 1.0 + nrm((DEPTH, n), 0.02)

    d = {}
    d['x_prompt'] = nrm((BATCH, SEQ, D_MODEL), 1.0)
    d['x_sample'] = nrm((DEC_BATCH, DEC_SEQ, D_MODEL), 1.0)
    d['mem_prompt'] = nrm((BATCH, N_MEM, D_MODEL), 1.0)
    d['mem_sample'] = nrm((DEC_BATCH, N_MEM, D_MODEL), 1.0)
    d['g_ffn1_pre'] = gain(D_MODEL)
    d['g_ffn1_post'] = gain(D_MODEL)
    d['w_ffn1_gate'] = nrm((DEPTH, D_MODEL, D_FF), D_MODEL ** -0.5)
    d['w_ffn1_up'] = nrm((DEPTH, D_MODEL, D_FF), D_MODEL ** -0.5)
    d['w_ffn1_down'] = nrm((DEPTH, D_FF, D_MODEL), D_FF ** -0.5)
    d['g_mix_pre'] = gain(D_MODEL)
    d['g_mix_post'] = gain(D_MODEL)
    d['w_in'] = nrm((DEPTH, D_MODEL, D_IN), D_MODEL ** -0.5)
    d['conv_dw_w'] = nrm((DEPTH, CONV_WIDTH, CONV_CH), CONV_WIDTH ** -0.5)
    d['conv_dw_b'] = nrm((DEPTH, CONV_CH), 0.02)
    d['conv_ln_g'] = gain(CONV_CH)
    d['conv_ln_b'] = nrm((DEPTH, CONV_CH), 0.02)
    d['g_q_lat'] = gain(Q_LORA_RANK)
    d['g_kv_lat'] = gain(KV_LORA_RANK)
    d['w_q_up'] = nrm((DEPTH, Q_LORA_RANK, MLA_HEADS * (QK_NOPE_DIM + QK_ROPE_DIM)), Q_LORA_RANK ** -0.5)
    d['w_kv_up'] = nrm((DEPTH, KV_LORA_RANK, MLA_HEADS * (QK_NOPE_DIM + V_HEAD_DIM)), KV_LORA_RANK ** -0.5)
    d['g_group_out'] = gain(D_MIX)
    d['w_out'] = nrm((DEPTH, D_MIX, D_MODEL), D_MIX ** -0.5)
    d['g_x_pre'] = gain(D_MODEL)
    d['g_x_post'] = gain(D_MODEL)
    d['g_mem'] = gain(D_MODEL)
    d['w_xq'] = nrm((DEPTH, D_MODEL, D_MODEL), D_MODEL ** -0.5)
    d['w_xk'] = nrm((DEPTH, D_MODEL, D_MODEL), D_MODEL ** -0.5)
    d['w_xv'] = nrm((DEPTH, D_MODEL, D_MODEL), D_MODEL ** -0.5)
    d['w_xo'] = nrm((DEPTH, D_MODEL, D_MODEL), D_MODEL ** -0.5)
    d['g_ffn2_pre'] = gain(D_MODEL)
    d['g_ffn2_post'] = gain(D_MODEL)
    d['w_ffn2_gate'] = nrm((DEPTH, D_MODEL, D_FF), D_MODEL ** -0.5)
    d['w_ffn2_up'] = nrm((DEPTH, D_MODEL, D_FF), D_MODEL ** -0.5)
    d['w_ffn2_down'] = nrm((DEPTH, D_FF, D_MODEL), D_FF ** -0.5)
    return d


def reference(x_prompt, x_sample, mem_prompt, mem_sample,
              g_ffn1_pre, g_ffn1_post, w_ffn1_gate, w_ffn1_up, w_ffn1_down,
              g_mix_pre, g_mix_post, w_in, conv_dw_w, conv_dw_b, conv_ln_g, conv_ln_b,
              g_q_lat, g_kv_lat, w_q_up, w_kv_up, g_group_out, w_out,
              g_x_pre, g_x_post, g_mem, w_xq, w_xk, w_xv, w_xo,
              g_ffn2_pre, g_ffn2_post, w_ffn2_gate, w_ffn2_up, w_ffn2_down):
    p = dict(
        g_ffn1_pre=g_ffn1_pre, g_ffn1_post=g_ffn1_post, w_ffn1_gate=w_ffn1_gate,
        w_ffn1_up=w_ffn1_up, w_ffn1_down=w_ffn1_down,
        g_mix_pre=g_mix_pre, g_mix_post=g_mix_post, w_in=w_in, conv_dw_w=conv_dw_w,
        conv_dw_b=conv_dw_b, conv_ln_g=conv_ln_g, conv_ln_b=conv_ln_b,
        g_q_lat=g_q_lat, g_kv_lat=g_kv_lat, w_q_up=w_q_up, w_kv_up=w_kv_up,
        g_group_out=g_group_out, w_out=w_out,
        g_x_pre=g_x_pre, g_x_post=g_x_post, g_mem=g_mem, w_xq=w_xq, w_xk=w_xk, w_xv=w_xv, w_xo=w_xo,
        g_ffn2_pre=g_ffn2_pre, g_ffn2_post=g_ffn2_post, w_ffn2_gate=w_ffn2_gate,
        w_ffn2_up=w_ffn2_up, w_ffn2_down=w_ffn2_down)
    y_prompt = encoder_trunk(x_prompt, mem_prompt, p)
    y_sample = encoder_trunk(x_sample, mem_sample, p)
    return (y_prompt, y_sample)
```

```python
from contextlib import ExitStack
import numpy as np
import concourse.bass as bass
import concourse.mybir as mybir
from concourse.bass_utils import run_bass_kernel_spmd

F32 = mybir.dt.float32
BF16 = mybir.dt.bfloat16
AF = mybir.ActivationFunctionType
ALU = mybir.AluOpType

D = 2048
DFF = 5632
NCH = 16
FCH = 44
T = 512
EPS = 1e-6
ENGS = ("pe", "act", "dve", "pool", "sp")
SEM_ROT = 12000


class Op:
    __slots__ = ("eng", "fn", "idx", "waits", "signal", "dma", "dcount", "sigsem", "sigval")

    def __init__(self, eng, fn, dma):
        self.eng = eng
        self.fn = fn
        self.dma = dma
        self.waits = []
        self.signal = False
        self.dcount = 0
        self.sigsem = None
        self.sigval = 0


class Prog:
    def __init__(self, nc, same_engine_sync=True):
        self.nc = nc
        self.ops = {e: [] for e in ENGS}
        self.last_w = {}
        self.readers = {}
        self.dma_cnt = {}
        self.waited = {e: {} for e in ENGS}
        self.same_engine_sync = same_engine_sync
        self.stack = ExitStack()

    def op(self, eng, fn, reads=(), writes=(), dma=None):
        o = Op(eng, fn, dma)
        o.idx = len(self.ops[eng])
        xr = [k for k in reads if isinstance(k, tuple) and k[0] == "ps"]
        if xr:
            reads = [k for k in reads if not (isinstance(k, tuple) and k[0] == "ps")]
            writes = list(writes) + xr
        need = {}

        def add(d, strong):
            if d.dma is not None:
                key = ("d", d.dma)
                v = d.dcount
            else:
                if d.eng == eng and (not strong or eng == "pe" or not self.same_engine_sync):
                    return
                key = d.eng
                v = d.idx
            cur = need.get(key)
            if cur is None or v > cur[0]:
                need[key] = (v, d)

        lw = self.last_w
        rd = self.readers
        for k in reads:
            w = lw.get(k)
            if w is not None:
                add(w, True)
        for k in writes:
            w = lw.get(k)
            if w is not None:
                add(w, True)
            for r in rd.get(k, ()):
                add(r, False)
        wd = self.waited[eng]
        for key, (v, d) in need.items():
            if d.dma is not None:
                if wd.get(key, 0) < v:
                    wd[key] = v
                    o.waits.append(d)
            else:
                if wd.get(key, -1) < v:
                    wd[key] = v
                    d.signal = True
                    o.waits.append(d)
        if dma is not None:
            c = self.dma_cnt.get(dma, 0) + 16
            self.dma_cnt[dma] = c
            o.dcount = c
        self.ops[eng].append(o)
        for k in reads:
            rd.setdefault(k, []).append(o)
        for k in writes:
            lw[k] = o
            rd[k] = []
        return o

    def emit(self, final_waits=()):
        nc = self.nc
        st = self.stack
        nsem = 0
        for e in ENGS:
            cnt = 0
            sem = None
            for o in self.ops[e]:
                if o.signal and o.dma is None:
                    if sem is None or cnt >= SEM_ROT:
                        sem = st.enter_context(nc.semaphore(f"s_{e}_{nsem}"))
                        nsem += 1
                        cnt = 0
                    cnt += 1
                    o.sigsem = sem
                    o.sigval = cnt
        dsems = {}
        for k in self.dma_cnt:
            dsems[k] = st.enter_context(nc.semaphore(f"d_{nsem}"))
            nsem += 1
        self.nsem = nsem

        def replay(e, eng):
            for o in self.ops[e]:
                for d in o.waits:
                    if d.dma is not None:
                        eng.wait_ge(dsems[d.dma], d.dcount)
                    else:
                        eng.wait_ge(d.sigsem, d.sigval)
                ins = o.fn(eng)
                if o.dma is not None:
                    ins.then_inc(dsems[o.dma], 16)
                elif o.signal:
                    ins.then_inc(o.sigsem, 1)
            if e == "sp":
                for d in final_waits:
                    eng.wait_ge(dsems[d.dma], d.dcount)

        with nc.Block() as block:
            @block.sync
            def _(eng):
                replay("sp", eng)

            @block.scalar
            def _(eng):
                replay("act", eng)

            @block.vector
            def _(eng):
                replay("dve", eng)

            @block.gpsimd
            def _(eng):
                replay("pool", eng)

            @block.tensor
            def _(eng):
                replay("pe", eng)


GV = {}
_off = 0
for _n, _c in [("ffn1_pre", 16), ("ffn1_post", 16), ("mix_pre", 16), ("mix_post", 16), ("x_pre", 16),
               ("x_post", 16), ("mem", 16), ("ffn2_pre", 16), ("ffn2_post", 16), ("group", 16),
               ("dw_b", 8), ("ln_g", 8), ("ln_b", 8), ("q_lat", 4), ("kv_lat", 2)]:
    GV[_n] = _off
    _off += _c
NGV = _off
WNAMES = ["w_ffn1_gate", "w_ffn1_up", "w_ffn1_down", "w_in", "w_q_up", "w_kv_up", "w_out",
          "w_xq", "w_xk", "w_xv", "w_xo", "w_ffn2_gate", "w_ffn2_up", "w_ffn2_down"]
WSHAPE = {"w_ffn1_gate": (D, DFF), "w_ffn1_up": (D, DFF), "w_ffn1_down": (DFF, D), "w_in": (D, 2880),
          "w_q_up": (512, 1536), "w_kv_up": (256, 2048), "w_out": (D, D), "w_xq": (D, D), "w_xk": (D, D),
          "w_xv": (D, D), "w_xo": (D, D), "w_ffn2_gate": (D, DFF), "w_ffn2_up": (D, DFF),
          "w_ffn2_down": (DFF, D)}


def build(NT, DEPTH, NMEM=1024):
    S = NT * T
    KC = S // 128
    nc = bass.Bass("TRN2", target_bir_lowering=False)
    P = Prog(nc)
    st = P.stack

    def din(name, shape, dt=F32):
        return nc.dram_tensor(name, list(shape), dt, kind="ExternalInput").ap()

    def dscr(name, shape, dt=F32):
        return nc.dram_tensor(name, list(shape), dt, kind="Internal").ap()

    xT = din("xT", [128, NCH, S])
    memT = din("memT", [128, NCH, NMEM])
    cs_d = din("cs", [64, 2, S])
    amask_d = din("amask", [128, NT * KC])
    hmask_d = din("hmask", [128, NT * 2])
    gv_d = din("gv", [128, DEPTH * NGV])
    cw_d = din("cw", [128, DEPTH * 8 * 31])
    Wd = {n: din(n, [DEPTH, WSHAPE[n][0], WSHAPE[n][1]]) for n in WNAMES}
    yT = nc.dram_tensor("yT", [128, NCH, S], F32, kind="ExternalOutput").ap()

    xs = dscr("xs", [128, NCH, S])
    hc_s = dscr("hc_s", [128, 8, S])
    qn_s = dscr("qn_s", [128, 8, S], BF16)
    qr_s = dscr("qr_s", [64, 8, S], BF16)
    ckv_s = dscr("ckv_s", [128, 2, S], BF16)
    kr_s = dscr("kr_s", [64, S], BF16)
    attn_s = dscr("attn_s", [128, 8, S])
    mhat_s = dscr("mhat_s", [128, NCH, NMEM])
    kx_s = dscr("kx_s", [128, NCH, NMEM], BF16)
    vx_s = dscr("vx_s", [128, NMEM // 128, D], BF16)

    def sb(name, shape, dt):
        return st.enter_context(nc.sbuf_tensor(name, list(shape), dt))

    X = sb("X", [128, NCH, T], F32)
    H = sb("H", [128, NCH, T], BF16)
    HID = sb("HID", [128, FCH, T], BF16)
    Fb = sb("F", [128, NCH, T], F32)
    NSLOT = 8
    UNIT = 2048
    WS = [sb(f"ws{i}", [128, UNIT], BF16) for i in range(NSLOT)]
    HCP = sb("HCP", [128, 8, T + 30], F32)
    ones = sb("ones", [128, 5, 128], BF16)
    gvt = sb("gvt", [128, DEPTH * NGV], F32)
    cwt = sb("cwt", [128, DEPTH * 8 * 31], F32)
    amask = sb("amask_t", [128, NT * KC], F32)
    hmask = sb("hmask_t", [128, NT * 2], F32)
    CS = sb("CS", [64, 2, T], F32)
    SQ = [sb(f"sq{i}", [128, T], BF16) for i in range(2)]
    TM = [sb(f"tm{i}", [128, T], F32) for i in range(4)]
    RS = [sb(f"rs{i}", [128, T], F32) for i in range(3)]
    PS = [st.enter_context(nc.psum_tensor(f"ps{i}", [128, T], F32)) for i in range(8)]
    ONE_IDX = {2048: 0, 1024: 1, 512: 2, 256: 3, 1: 4}

    A = P.op
    cnt = {"bank": 0, "w": 0, "sq": 0, "tm": 0}

    def XK(a, b=None):
        return [("X", c) for c in (range(a, b) if b is not None else [a])]

    def HK(a, b=None):
        return [("H", c) for c in (range(a, b) if b is not None else [a])]

    def DK(a, b=None):
        return [("HID", c) for c in (range(a, b) if b is not None else [a])]

    def FK(a, b=None):
        return [("F", c) for c in (range(a, b) if b is not None else [a])]

    def bank(pool=(0, 1, 2, 3, 4, 5, 6)):
        i = pool[cnt["bank"] % len(pool)]
        cnt["bank"] += 1
        return i

    def mm(out, lhsT, rhs, start, stop, r, w):
        A("pe", lambda e: e.matmul(out, lhsT=lhsT, rhs=rhs, start=start, stop=stop), reads=r, writes=w)

    def act(out, in_, func, r, w, bias=None, scale=None):
        kw = {}
        if bias is not None:
            kw["bias"] = bias
        if scale is not None:
            kw["scale"] = scale
        A("act", lambda e: e.activation(out=out, in_=in_, func=func, **kw), reads=r, writes=w)

    def acopy(out, in_, r, w):
        A("act", lambda e: e.copy(out=out, in_=in_), reads=r, writes=w)

    def vcopy(out, in_, r, w):
        A("dve", lambda e: e.tensor_copy(out=out, in_=in_), reads=r, writes=w)

    def vtt(out, a, b, op, r, w):
        A("dve", lambda e: e.tensor_tensor(out=out, in0=a, in1=b, op=op), reads=r, writes=w)

    def vstt(out, in0, scalar, in1, op0, op1, r, w):
        A("dve", lambda e: e.scalar_tensor_tensor(out=out, in0=in0, scalar=scalar, in1=in1, op0=op0, op1=op1),
          reads=r, writes=w)

    def vts(out, in0, s1, s2, op0, op1, r, w):
        if s2 is None:
            A("dve", lambda e: e.tensor_scalar(out=out, in0=in0, scalar1=s1, scalar2=None, op0=op0), reads=r, writes=w)
        else:
            A("dve", lambda e: e.tensor_scalar(out=out, in0=in0, scalar1=s1, scalar2=s2, op0=op0, op1=op1),
              reads=r, writes=w)

    def dma(eng, out, in_, r, w, key):
        return A(eng, lambda e: e.dma_start(out=out, in_=in_), reads=r, writes=w, dma=key)

    def wload(Wl, kc0, nkc, n0, nw):
        assert nkc * nw <= UNIT, (nkc, nw)
        s = cnt["w"] % NSLOT
        cnt["w"] += 1
        view = WS[s][:, 0:nkc * nw].rearrange("p (k n) -> p k n", k=nkc)
        src = Wl[kc0 * 128:(kc0 + nkc) * 128, n0:n0 + nw].rearrange("(k p) n -> p k n", p=128)
        dma("pool", view, src, [], [("w", s)], ("w", s))
        return view, ("w", s)

    def gcol(l, name, c):
        o = l * NGV + GV[name] + c
        return gvt[:, o:o + 1]

    for i, v in enumerate([1.0 / 2048, 1.0 / 1024, 1.0 / 512, 1.0 / 256, 1.0]):
        A("dve", lambda e, i=i, v=v: e.memset(ones[:, i, :], v), writes=["ones"])
    dma("sp", gvt[:], gv_d, [], ["gvt"], "c_gv")
    dma("sp", cwt[:], cw_d, [], ["cwt"], "c_cw")
    dma("sp", amask[:], amask_d, [], ["amask"], "c_am")
    dma("sp", hmask[:], hmask_d, [], ["hmask"], "c_hm")

    def stats(srcs, rkeys, Dn, out_rs, np_=128):
        b = bank()
        n = len(srcs)
        for c, (s_ap, rk) in enumerate(zip(srcs, rkeys)):
            q = cnt["sq"] % 2
            cnt["sq"] += 1
            act(SQ[q][:], s_ap, AF.Square, [rk], [("sq", q)])
            mm(PS[b][:], ones[:, ONE_IDX[Dn], :], SQ[q][:], c == 0, c == n - 1, ["ones", ("sq", q)], [("ps", b)])
        rk_ = out_rs[:].name
        act(out_rs[:], PS[b][:], AF.Sqrt, [("ps", b)], [rk_], bias=EPS, scale=1.0)
        A("dve", lambda e: e.reciprocal(out=out_rs[:], in_=out_rs[:]), reads=[rk_], writes=[rk_])

    def norm_to_H(l, gname, src, srckeys, n, Dn, rs, hoff=0):
        stats([src[:, c, :] for c in range(n)], srckeys, Dn, rs)
        for c in range(n):
            vstt(H[:, hoff + c, :], src[:, c, :], gcol(l, gname, c), rs[:], ALU.mult, ALU.mult,
                 [srckeys[c], rs[:].name, "gvt"], HK(hoff + c))

    def linear_fm(Wl, nkc, col0, nchunks, rhs, rkeys, evac, slab_cols=256):
        kpu = UNIT // slab_cols
        nun = (nkc + kpu - 1) // kpu
        cps = slab_cols // 128
        for s0 in range(0, nchunks, cps):
            ncs = min(cps, nchunks - s0)
            units = [wload(Wl, u * kpu, min(kpu, nkc - u * kpu), col0 + s0 * 128, ncs * 128) for u in range(nun)]
            for j in range(ncs):
                b = bank()
                for kc in range(nkc):
                    v, wk = units[kc // kpu]
                    mm(PS[b][:], v[:, kc % kpu, j * 128:(j + 1) * 128], rhs(kc), kc == 0, kc == nkc - 1,
                       [wk, rkeys(kc)], [("ps", b)])
                evac(s0 + j, b)

    def residual_update(l, gname, coef, rs):
        stats([Fb[:, c, :] for c in range(NCH)], FK(0, NCH), 2048, rs)
        if coef != 1.0:
            vts(rs[:], rs[:], coef, None, ALU.mult, None, [rs[:].name], [rs[:].name])
        for c in range(NCH):
            q = cnt["tm"] % 4
            cnt["tm"] += 1
            vstt(TM[q][:], Fb[:, c, :], gcol(l, gname, c), rs[:], ALU.mult, ALU.mult,
                 [("F", c), rs[:].name, "gvt"], [("tm", q)])
            vtt(X[:, c, :], X[:, c, :], TM[q][:], ALU.add, [("X", c), ("tm", q)], XK(c))

    def ffn(l, which):
        pre, post = f"ffn{which}_pre", f"ffn{which}_post"
        Wg, Wu, Wdn = Wd[f"w_ffn{which}_gate"][l], Wd[f"w_ffn{which}_up"][l], Wd[f"w_ffn{which}_down"][l]
        norm_to_H(l, pre, X, XK(0, NCH), NCH, 2048, RS[0])
        pairs = ((0, 1), (2, 3), (4, 5))
        pc = 0
        for s in range(FCH // 2):
            gu = [wload(Wg, 0, 8, s * 256, 256), wload(Wg, 8, 8, s * 256, 256),
                  wload(Wu, 0, 8, s * 256, 256), wload(Wu, 8, 8, s * 256, 256)]
            for j in range(2):
                ba, bb = pairs[pc % 3]
                pc += 1
                for kc in range(NCH):
                    v, wk = gu[kc // 8]
                    mm(PS[ba][:], v[:, kc % 8, j * 128:(j + 1) * 128], H[:, kc, :], kc == 0, kc == NCH - 1,
                       [wk, ("H", kc)], [("ps", ba)])
                for kc in range(NCH):
                    v, wk = gu[2 + kc // 8]
                    mm(PS[bb][:], v[:, kc % 8, j * 128:(j + 1) * 128], H[:, kc, :], kc == 0, kc == NCH - 1,
                       [wk, ("H", kc)], [("ps", bb)])
                q = cnt["tm"] % 4
                cnt["tm"] += 1
                act(TM[q][:], PS[ba][:], AF.Silu, [("ps", ba)], [("tm", q)])
                vtt(HID[:, s * 2 + j, :], TM[q][:], PS[bb][:], ALU.mult, [("tm", q), ("ps", bb)], DK(s * 2 + j))

        def ev(j, b):
            acopy(Fb[:, j, :], PS[b][:], [("ps", b)], FK(j))
        linear_fm(Wdn, FCH, 0, NCH, lambda kc: HID[:, kc, :], lambda kc: ("HID", kc), ev)
        residual_update(l, post, 0.5, RS[1])

    def rope_pair(braw, bswp, out_ap, outkeys):
        vtt(TM[0][0:64, :], PS[braw][0:64, :], CS[:, 0, :], ALU.mult, [("ps", braw), "CS"], [("tm", 0)])
        vtt(TM[1][0:64, :], PS[bswp][0:64, :], CS[:, 1, :], ALU.mult, [("ps", bswp), "CS"], [("tm", 1)])
        vtt(out_ap, TM[0][0:64, :], TM[1][0:64, :], ALU.add, [("tm", 0), ("tm", 1)], outkeys)

    def rope_mms(units_of, col, nkc, rhs, rkeys):
        braw, bswp = bank(), bank()
        for kc in range(nkc):
            v, wk, off = units_of(kc)
            mm(PS[braw][0:64, :], v[:, off, col:col + 64], rhs(kc), kc == 0, kc == nkc - 1,
               [wk, rkeys(kc)], [("ps", braw)])
        for kc in range(nkc):
            v, wk, off = units_of(kc)
            mm(PS[bswp][0:32, :], v[:, off, col + 32:col + 64], rhs(kc), kc == 0, kc == nkc - 1,
               [wk, rkeys(kc)], [("ps", bswp)])
        for kc in range(nkc):
            v, wk, off = units_of(kc)
            mm(PS[bswp][32:64, :], v[:, off, col:col + 32], rhs(kc), kc == 0, kc == nkc - 1,
               [wk, rkeys(kc)], [("ps", bswp)])
        return braw, bswp

    for j in range(NMEM // T):
        dma("sp", X[:], memT[:, :, j * T:(j + 1) * T], [], XK(0, NCH), "ld_x")
        stats([X[:, c, :] for c in range(NCH)], XK(0, NCH), 2048, RS[0])
        for c in range(NCH):
            vtt(Fb[:, c, :], X[:, c, :], RS[0][:], ALU.mult, [("X", c), "rs0"], FK(c))
        dma("sp", mhat_s[:, :, j * T:(j + 1) * T], Fb[:], FK(0, NCH), [("mhat", j)], "st_f")

    out_ops = []
    for l in range(DEPTH):
        for t in range(NT):
            t0 = t * T
            src = xT if l == 0 else xs
            dma("sp", X[:], src[:, :, t0:t0 + T], [("xs", t)] if l > 0 else [], XK(0, NCH), "ld_x")
            dma("sp", CS[:], cs_d[:, :, t0:t0 + T], [], ["CS"], "ld_cs")
            ffn(l, 1)
            norm_to_H(l, "mix_pre", X, XK(0, NCH), NCH, 2048, RS[0])
            Wi = Wd["w_in"][l]
            for s in range(4):
                au = [wload(Wi, 0, 8, s * 256, 256), wload(Wi, 8, 8, s * 256, 256),
                      wload(Wi, 0, 8, 1024 + s * 256, 256), wload(Wi, 8, 8, 1024 + s * 256, 256)]
                for j in range(2):
                    ba, bb = bank(), bank()
                    for kc in range(NCH):
                        v, wk = au[kc // 8]
                        mm(PS[ba][:], v[:, kc % 8, j * 128:(j + 1) * 128], H[:, kc, :], kc == 0, kc == NCH - 1,
                           [wk, ("H", kc)], [("ps", ba)])
                    for kc in range(NCH):
                        v, wk = au[2 + kc // 8]
                        mm(PS[bb][:], v[:, kc % 8, j * 128:(j + 1) * 128], H[:, kc, :], kc == 0, kc == NCH - 1,
                           [wk, ("H", kc)], [("ps", bb)])
                    q = cnt["tm"] % 4
                    cnt["tm"] += 1
                    act(TM[q][:], PS[bb][:], AF.Sigmoid, [("ps", bb)], [("tm", q)])
                    vtt(Fb[:, s * 2 + j, :], TM[q][:], PS[ba][:], ALU.mult, [("tm", q), ("ps", ba)], FK(s * 2 + j))
            dma("sp", hc_s[:, :, t0:t0 + T], Fb[:, 0:8, :], FK(0, 8), [("hc", t)], "st_hc")
            def ev_lat(j, b):
                acopy(Fb[:, 8 + j, :], PS[b][:], [("ps", b)], FK(8 + j))
            linear_fm(Wi, NCH, 2048, 4, lambda kc: H[:, kc, :], lambda kc: ("H", kc), ev_lat)
            u2 = [wload(Wi, 4 * u, 4, 2560, 320) for u in range(4)]
            for j in range(2):
                b = bank()
                for kc in range(NCH):
                    v, wk = u2[kc // 4]
                    mm(PS[b][:], v[:, kc % 4, j * 128:(j + 1) * 128], H[:, kc, :], kc == 0, kc == NCH - 1,
                       [wk, ("H", kc)], [("ps", b)])
                acopy(Fb[:, 12 + j, :], PS[b][:], [("ps", b)], FK(12 + j))
            braw, bswp = rope_mms(lambda kc: (u2[kc // 4][0], u2[kc // 4][1], kc % 4), 256, NCH,
                                  lambda kc: H[:, kc, :], lambda kc: ("H", kc))
            rope_pair(braw, bswp, HID[0:64, 18, :], DK(18))
            dma("sp", kr_s[:, t0:t0 + T], HID[0:64, 18, :], DK(18), [("kr", t)], "st_kr")
            stats([Fb[:, 12 + c, :] for c in range(2)], FK(12, 14), 256, RS[1])
            for c in range(2):
                vstt(HID[:, 16 + c, :], Fb[:, 12 + c, :], gcol(l, "kv_lat", c), RS[1][:], ALU.mult, ALU.mult,
                     [("F", 12 + c), "rs1", "gvt"], DK(16 + c))
            dma("sp", ckv_s[:, :, t0:t0 + T], HID[:, 16:18, :], DK(16, 18), [("ckv", t)], "st_ckv")
            stats([Fb[:, 8 + c, :] for c in range(4)], FK(8, 12), 512, RS[2])
            for c in range(4):
                vstt(H[:, c, :], Fb[:, 8 + c, :], gcol(l, "q_lat", c), RS[2][:], ALU.mult, ALU.mult,
                     [("F", 8 + c), "rs2", "gvt"], HK(c))
            Wq = Wd["w_q_up"][l]
            qu = [wload(Wq, 0, 4, 384 * u, 384) for u in range(4)]
            for h in range(8):
                v, wk = qu[h // 2]
                c0 = (h % 2) * 192
                b = bank()
                for kc in range(4):
                    mm(PS[b][:], v[:, kc, c0:c0 + 128], H[:, kc, :], kc == 0, kc == 3, [wk, ("H", kc)], [("ps", b)])
                acopy(HID[:, h, :], PS[b][:], [("ps", b)], DK(h))
                braw, bswp = rope_mms(lambda kc, v=v, wk=wk: (v, wk, kc), c0 + 128, 4,
                                      lambda kc: H[:, kc, :], lambda kc: ("H", kc))
                rope_pair(braw, bswp, HID[0:64, 8 + h, :], DK(8 + h))
            dma("sp", qn_s[:, :, t0:t0 + T], HID[:, 0:8, :], DK(0, 8), [("qn", t)], "st_qn")
            dma("sp", qr_s[:, :, t0:t0 + T], HID[0:64, 8:16, :], DK(8, 16), [("qr", t)], "st_qr")
            dma("sp", xs[:, :, t0:t0 + T], X[:], XK(0, NCH), [("xs", t)], "st_x")

        for j in range(NMEM // T):
            dma("sp", X[:], mhat_s[:, :, j * T:(j + 1) * T], [("mhat", j)], XK(0, NCH), "ld_x")
            for c in range(NCH):
                vts(H[:, c, :], X[:, c, :], gcol(l, "mem", c), None, ALU.mult, None, [("X", c), "gvt"], HK(c))

            def ev_kx(jj, b):
                acopy(HID[:, jj, :], PS[b][:], [("ps", b)], DK(jj))
            linear_fm(Wd["w_xk"][l], NCH, 0, NCH, lambda kc: H[:, kc, :], lambda kc: ("H", kc), ev_kx)
            dma("sp", kx_s[:, :, j * T:(j + 1) * T], HID[:, 0:16, :], DK(0, 16), [("kx", j)], "st_kx")
            VXS = HID[:, 16:32, :].rearrange("p c t -> p (c t)").rearrange("p (m f) -> p m f", m=4)
            Wv = Wd["w_xv"][l]
            for s in range(8):
                vu = [wload(Wv, 0, 8, s * 256, 256), wload(Wv, 8, 8, s * 256, 256)]
                for m in range(4):
                    b = bank()
                    for kc in range(NCH):
                        v, wk = vu[kc // 8]
                        mm(PS[b][:, 0:256], H[:, kc, m * 128:(m + 1) * 128], v[:, kc % 8, :], kc == 0, kc == NCH - 1,
                           [wk, ("H", kc)], [("ps", b)])
                    acopy(VXS[:, m, s * 256:(s + 1) * 256], PS[b][:, 0:256], [("ps", b)], DK(16 + 4 * m + s // 2))
            dma("sp", vx_s[:, 4 * j:4 * j + 4, :], VXS, DK(16, 32), [("vx", j)], "st_vx")

        KRv = X[:].rearrange("p c t -> p (c t)")
        KRb = KRv.bitcast(BF16)[:, 0:S]
        nxk = (S * 2 + 2047) // 2048
        dma("sp", KRb[0:64, :], kr_s[:, :], [("kr", t) for t in range(NT)], XK(0, nxk), "ld_x")
        KH = HID[:, 0:16, :].rearrange("p c t -> p (c t)")
        VH = HID[:, 16:32, :].rearrange("p c t -> p (c t)").rearrange("p (k d) -> p k d", d=128)
        Wkv = Wd["w_kv_up"][l]
        scale = (128 + 64) ** -0.5
        for h in range(8):
            wv, wkk = wload(Wkv, 0, 2, h * 256, 256)
            for kt in range(NT):
                cp = 32 + 2 * (kt % 2)
                dma("sp", HID[:, cp:cp + 2, :], ckv_s[:, :, kt * T:(kt + 1) * T], [("ckv", kt)], DK(cp, cp + 2), f"ld_cp{kt % 2}")
                b = bank((0, 1, 2, 3))
                for kc in range(2):
                    mm(PS[b][:], wv[:, kc, 0:128], HID[:, cp + kc, :], kc == 0, kc == 1, [wkk] + DK(cp + kc), [("ps", b)])
                acopy(KH[:, kt * T:(kt + 1) * T], PS[b][:], [("ps", b)], DK(kt))
                b = bank((0, 1, 2, 3))
                for sub in range(4):
                    for kc in range(2):
                        mm(PS[b][:, sub * 128:(sub + 1) * 128], HID[:, cp + kc, sub * 128:(sub + 1) * 128],
                           wv[:, kc, 128:256], kc == 0, kc == 1, [wkk] + DK(cp + kc), [("ps", b)])
                vcopy(VH[:, 4 * kt:4 * kt + 4, :], PS[b][:].rearrange("p (k d) -> p k d", d=128), [("ps", b)], DK(16 + kt))
            for qt in range(NT):
                qb = qt % 2
                dma("sp", HID[:, 36 + qb, :], qn_s[:, h, qt * T:(qt + 1) * T], [("qn", qt)], DK(36 + qb), f"ld_qn{qb}")
                dma("sp", HID[0:64, 38 + qb, :], qr_s[:, h, qt * T:(qt + 1) * T], [("qr", qt)], DK(38 + qb), f"ld_qr{qb}")
                bo, bd = ((4, 5), (6, 7))[qt % 2]
                sbank = {}

                def smm(kc):
                    b = bank((0, 1, 2, 3))
                    sbank[kc] = b
                    mm(PS[b][:], KH[:, kc * 128:(kc + 1) * 128], HID[:, 36 + qb, :], True, False,
                       DK(kc // 4) + DK(36 + qb), [("ps", b)])
                    mm(PS[b][:], KRb[0:64, kc * 128:(kc + 1) * 128], HID[0:64, 38 + qb, :], False, True,
                       XK((kc * 256) // 2048) + DK(38 + qb), [("ps", b)])

                smm(0)
                if KC > 1:
                    smm(1)
                for kc in range(KC):
                    b = sbank.pop(kc)
                    pt = 40 + kc % 4
                    act(HID[:, pt, :], PS[b][:], AF.Exp, [("ps", b), "amask"], DK(pt),
                        bias=amask[:, qt * KC + kc:qt * KC + kc + 1], scale=scale)
                    mm(PS[bo][:], VH[:, kc, :], HID[:, pt, :], kc == 0, kc == KC - 1, DK(16 + kc // 4) + DK(pt), [("ps", bo)])
                    mm(PS[bd][:], ones[:, 4, :], HID[:, pt, :], kc == 0, kc == KC - 1, ["ones"] + DK(pt), [("ps", bd)])
                    if kc + 2 < KC:
                        smm(kc + 2)
                q = cnt["tm"] % 4
                cnt["tm"] += 1
                A("dve", lambda e, q=q, bd=bd: e.reciprocal(out=TM[q][:], in_=PS[bd][:]), reads=[("ps", bd)], writes=[("tm", q)])
                q2 = cnt["tm"] % 4
                cnt["tm"] += 1
                vtt(TM[q2][:], PS[bo][:], TM[q][:], ALU.mult, [("ps", bo), ("tm", q)], [("tm", q2)])
                dma("sp", attn_s[:, h, qt * T:(qt + 1) * T], TM[q2][:], [("tm", q2)], [("attn", qt)], f"st_at{q2}")

        for t in range(NT):
            t0 = t * T
            dma("sp", X[:], xs[:, :, t0:t0 + T], [("xs", t)], XK(0, NCH), "ld_x")
            dma("sp", HCP[:, :, 15:15 + T], hc_s[:, :, t0:t0 + T], [("hc", t)], ["HCPm"], "ld_hc")
            if t > 0:
                dma("sp", HCP[:, :, 0:15], hc_s[:, :, t0 - 15:t0], [("hc", t - 1)], ["HCPl"], "ld_hl")
                vts(HCP[:, :, 0:15], HCP[:, :, 0:15], hmask[:, 2 * t:2 * t + 1], None, ALU.mult, None,
                    ["HCPl", "hmask"], ["HCPl"])
            else:
                A("dve", lambda e: e.memset(HCP[:, :, 0:15], 0.0), writes=["HCPl"])
            if t < NT - 1:
                dma("sp", HCP[:, :, 15 + T:30 + T], hc_s[:, :, t0 + T:t0 + T + 15], [("hc", t + 1)], ["HCPr"], "ld_hr")
                vts(HCP[:, :, 15 + T:30 + T], HCP[:, :, 15 + T:30 + T], hmask[:, 2 * t + 1:2 * t + 2], None,
                    ALU.mult, None, ["HCPr", "hmask"], ["HCPr"])
            else:
                A("dve", lambda e: e.memset(HCP[:, :, 15 + T:30 + T], 0.0), writes=["HCPr"])
            for cc in range(8):
                wb = (l * 8 + cc) * 31
                vts(Fb[:, cc, :], HCP[:, cc, 0:T], cwt[:, wb:wb + 1], gcol(l, "dw_b", cc), ALU.mult, ALU.add,
                    ["HCPm", "HCPl", "HCPr", "cwt", "gvt"], FK(cc))
                for tap in range(1, 31):
                    vstt(Fb[:, cc, :], HCP[:, cc, tap:tap + T], cwt[:, wb + tap:wb + tap + 1], Fb[:, cc, :],
                         ALU.mult, ALU.add, ["HCPm", "HCPl", "HCPr", "cwt", ("F", cc)], FK(cc))
            bm, bv = bank(), bank()
            for c in range(8):
                q = cnt["sq"] % 2
                cnt["sq"] += 1
                acopy(SQ[q][:], Fb[:, c, :], FK(c), [("sq", q)])
                mm(PS[bm][:], ones[:, 1, :], SQ[q][:], c == 0, c == 7, ["ones", ("sq", q)], [("ps", bm)])
                q = cnt["sq"] % 2
                cnt["sq"] += 1
                act(SQ[q][:], Fb[:, c, :], AF.Square, FK(c), [("sq", q)])
                mm(PS[bv][:], ones[:, 1, :], SQ[q][:], c == 0, c == 7, ["ones", ("sq", q)], [("ps", bv)])
            vcopy(RS[0][:], PS[bm][:], [("ps", bm)], ["rs0"])
            vtt(RS[1][:], RS[0][:], RS[0][:], ALU.mult, ["rs0"], ["rs1"])
            vtt(RS[1][:], PS[bv][:], RS[1][:], ALU.subtract, [("ps", bv), "rs1"], ["rs1"])
            act(RS[1][:], RS[1][:], AF.Sqrt, ["rs1"], ["rs1"], bias=EPS, scale=1.0)
            A("dve", lambda e: e.reciprocal(out=RS[1][:], in_=RS[1][:]), reads=["rs1"], writes=["rs1"])
            for c in range(8):
                vtt(Fb[:, c, :], Fb[:, c, :], RS[0][:], ALU.subtract, [("F", c), "rs0"], FK(c))
                vtt(Fb[:, c, :], Fb[:, c, :], RS[1][:], ALU.mult, [("F", c), "rs1"], FK(c))
                act(Fb[:, c, :], Fb[:, c, :], AF.Silu, [("F", c), "gvt"], FK(c),
                    bias=gcol(l, "ln_b", c), scale=gcol(l, "ln_g", c))
            norm_to_H(l, "group", Fb, FK(0, 8), 8, 1024, RS[2])
            dma("sp", Fb[:, 8:16, :], attn_s[:, :, t0:t0 + T], [("attn", t)], FK(8, 16), "ld_at")
            stats([Fb[:, 8 + c, :] for c in range(8)], FK(8, 16), 1024, RS[0])
            for c in range(8):
                vstt(H[:, 8 + c, :], Fb[:, 8 + c, :], gcol(l, "group", 8 + c), RS[0][:], ALU.mult, ALU.mult,
                     [("F", 8 + c), "rs0", "gvt"], HK(8 + c))

            def ev_f(j, b):
                acopy(Fb[:, j, :], PS[b][:], [("ps", b)], FK(j))
            linear_fm(Wd["w_out"][l], NCH, 0, NCH, lambda kc: H[:, kc, :], lambda kc: ("H", kc), ev_f)
            residual_update(l, "mix_post", 1.0, RS[1])
            norm_to_H(l, "x_pre", X, XK(0, NCH), NCH, 2048, RS[0])

            def ev_q(j, b):
                acopy(HID[:, j, :], PS[b][:], [("ps", b)], DK(j))
            linear_fm(Wd["w_xq"][l], NCH, 0, NCH, lambda kc: H[:, kc, :], lambda kc: ("H", kc), ev_q)
            m = t // (NT // 4)
            KXT = HID[:, 16:24, :].rearrange("p c t -> p (c t)").rearrange("p (c k) -> p c k", c=16)
            VXT = HID[:, 24:32, :].rearrange("p c t -> p (c t)").rearrange("p (k f) -> p k f", k=2)
            dma("sp", KXT, kx_s[:, :, 256 * m:256 * m + 256], [("kx", (256 * m) // T)], DK(16, 24), "ld_kx")
            dma("sp", VXT, vx_s[:, 2 * m:2 * m + 2, :], [("vx", (256 * m) // T)], DK(24, 32), "ld_vx")
            xscale = 512 ** -0.5
            for h4 in range(4):
                for km in range(2):
                    b = bank()
                    for dc in range(4):
                        mm(PS[b][:], KXT[:, 4 * h4 + dc, km * 128:(km + 1) * 128], HID[:, 4 * h4 + dc, :],
                           dc == 0, dc == 3, DK(16, 24) + DK(4 * h4 + dc), [("ps", b)])
                    act(HID[:, 32 + km, :], PS[b][:], AF.Exp, [("ps", b)], DK(32 + km), scale=xscale)
                bd = bank()
                for km in range(2):
                    mm(PS[bd][:], ones[:, 4, :], HID[:, 32 + km, :], km == 0, km == 1, ["ones"] + DK(32 + km), [("ps", bd)])
                q = cnt["tm"] % 4
                cnt["tm"] += 1
                A("dve", lambda e, q=q, bd=bd: e.reciprocal(out=TM[q][:], in_=PS[bd][:]), reads=[("ps", bd)], writes=[("tm", q)])
                for dc in range(4):
                    b = bank()
                    f0 = (4 * h4 + dc) * 128
                    for km in range(2):
                        mm(PS[b][:], VXT[:, km, f0:f0 + 128], HID[:, 32 + km, :], km == 0, km == 1,
                           DK(24, 32) + DK(32 + km), [("ps", b)])
                    vtt(H[:, 4 * h4 + dc, :], PS[b][:], TM[q][:], ALU.mult, [("ps", b), ("tm", q)], HK(4 * h4 + dc))
            linear_fm(Wd["w_xo"][l], NCH, 0, NCH, lambda kc: H[:, kc, :], lambda kc: ("H", kc), ev_f)
            residual_update(l, "x_post", 1.0, RS[1])
            ffn(l, 2)
            if l == DEPTH - 1:
                out_ops.append(dma("sp", yT[:, :, t0:t0 + T], X[:], XK(0, NCH), [("y", t)], "st_x"))
            else:
                dma("sp", xs[:, :, t0:t0 + T], X[:], XK(0, NCH), [("xs", t)], "st_x")

    P.emit(final_waits=out_ops[-1:])
    st.close()
    return nc, P


def fm(a):
    n, f = a.shape
    return np.ascontiguousarray(a.T.reshape(f // 128, 128, n).transpose(1, 0, 2))


def unfm(a):
    p, c, n = a.shape
    return np.ascontiguousarray(a.transpose(1, 0, 2).reshape(c * p, n).T)


def colvec(v):
    return np.ascontiguousarray(v.reshape(-1, 128).T)


def rope_table(pos):
    inv = 1.0 / (10000.0 ** (np.arange(0, 64, 2, dtype=np.float32) / np.float32(64)))
    ang = pos.astype(np.float32)[:, None] * inv[None, :].astype(np.float32)
    ang = np.concatenate([ang, ang], -1).astype(np.float32)
    cos = np.cos(ang).astype(np.float32).T
    sin = np.sin(ang).astype(np.float32).T.copy()
    sin[0:32] *= -1.0
    return np.ascontiguousarray(np.stack([cos, sin], 1)).astype(np.float32)


def make_core_inputs(seqs, mems, NT, p, DEPTH):
    S = NT * T
    KC = S // 128
    x = np.concatenate(seqs, 0)
    assert x.shape[0] == S
    lens = [s.shape[0] for s in seqs]
    starts = np.cumsum([0] + lens)
    pos = np.concatenate([np.arange(n) for n in lens])
    blk = np.concatenate([np.full(n, i) for i, n in enumerate(lens)])
    am = np.zeros((NT, KC), np.float32)
    for qt in range(NT):
        for kc in range(KC):
            if blk[qt * T] != blk[kc * 128]:
                am[qt, kc] = -30000.0
    hm = np.ones((NT, 2), np.float32)
    for t in range(NT):
        if t > 0 and blk[t * T - 1] != blk[t * T]:
            hm[t, 0] = 0.0
        if t < NT - 1 and blk[t * T + T] != blk[t * T + T - 1]:
            hm[t, 1] = 0.0
    d = {
        "xT": fm(x),
        "memT": fm(np.concatenate(mems, 0)),
        "cs": rope_table(pos),
        "amask": np.ascontiguousarray(np.broadcast_to(am.reshape(1, -1), (128, NT * KC))),
        "hmask": np.ascontiguousarray(np.broadcast_to(hm.reshape(1, -1), (128, NT * 2))),
    }
    return d


def shared_inputs(p, DEPTH):
    gv = np.zeros((128, DEPTH, NGV), np.float32)
    names = {"ffn1_pre": "g_ffn1_pre", "ffn1_post": "g_ffn1_post", "mix_pre": "g_mix_pre", "mix_post": "g_mix_post",
             "x_pre": "g_x_pre", "x_post": "g_x_post", "mem": "g_mem", "ffn2_pre": "g_ffn2_pre",
             "ffn2_post": "g_ffn2_post", "group": "g_group_out", "dw_b": "conv_dw_b", "ln_g": "conv_ln_g",
             "ln_b": "conv_ln_b", "q_lat": "g_q_lat", "kv_lat": "g_kv_lat"}
    for k, src in names.items():
        for l in range(DEPTH):
            cv = colvec(np.asarray(p[src][l], np.float32))
            gv[:, l, GV[k]:GV[k] + cv.shape[1]] = cv
    cw = np.zeros((128, DEPTH, 8, 31), np.float32)
    for l in range(DEPTH):
        w = np.asarray(p["conv_dw_w"][l], np.float32)
        cw[:, l] = w.T.reshape(8, 128, 31).transpose(1, 0, 2)
    d = {"gv": gv.reshape(128, -1), "cw": cw.reshape(128, -1)}
    for n in WNAMES:
        d[n] = np.ascontiguousarray(np.asarray(p[n][:DEPTH], np.float32))
    return d


_CACHE = {}


def run_model(core_seqs, core_mems, p, NT, DEPTH, ncores=8):
    key = (NT, DEPTH)
    if key not in _CACHE:
        _CACHE[key] = build(NT, DEPTH)
    nc, _ = _CACHE[key]
    sh = shared_inputs(p, DEPTH)
    S = NT * T
    zero_seq = [np.zeros((S, D), np.float32)]
    zero_mem = [np.zeros((256, D), np.float32)] * 4
    in_maps = []
    for c in range(ncores):
        if c in core_seqs:
            d = make_core_inputs(core_seqs[c], core_mems[c], NT, p, DEPTH)
        else:
            d = make_core_inputs(zero_seq, zero_mem, NT, p, DEPTH)
        d.update(sh)
        in_maps.append(d)
    res = run_bass_kernel_spmd(nc, in_maps, core_ids=list(range(ncores)))
    return {c: unfm(np.asarray(res.results[c]["yT"])) for c in core_seqs}


def kernel(**inputs):
    p = {k: np.asarray(v) for k, v in inputs.items()}
    xp, xsm = p["x_prompt"], p["x_sample"]
    mp, ms = p["mem_prompt"], p["mem_sample"]
    NT = 16
    core_seqs = {0: [xsm[0]], 2: [xsm[1]], 4: [xp[i] for i in range(4)]}
    core_mems = {0: [ms[0]] * 4, 2: [ms[1]] * 4, 4: [mp[i] for i in range(4)]}
    pnames = [k for k in p if k not in ("x_prompt", "x_sample", "mem_prompt", "mem_sample")]
    for l in range(4):
        pl = {k: p[k][l:l + 1] for k in pnames}
        out = run_model(core_seqs, core_mems, pl, NT, 1)
        core_seqs = {0: [out[0]], 2: [out[2]], 4: [out[4][i * 2048:(i + 1) * 2048] for i in range(4)]}
    y_sample = np.stack([out[0], out[2]], 0).astype(np.float32)
    y_prompt = out[4].reshape(4, 2048, D).astype(np.float32)
    return (y_prompt, y_sample)
```
